# Optimizing a Trainium2 kernel written in Bass

```python
import math
import jax
import jax.numpy as jnp
from jax import lax
import numpy as np

D_MODEL = 1024
BATCH = 8
SEQ = 4096
DEPTH = 1

MEM_LEN = 256
NORM_EPS = 1e-6
NEG_INF = -1e30
FORCE_SCORE = 1e9

SSD_D_INNER = 2 * D_MODEL
SSD_HEAD_DIM = 64
SSD_HEADS = SSD_D_INNER // SSD_HEAD_DIM
SSD_GROUPS = 8
SSD_STATE = 128
SSD_CONV = 4
SSD_CHUNK = 128
SSD_CONV_DIM = SSD_D_INNER + 2 * SSD_GROUPS * SSD_STATE

NSA_HEADS = 16
NSA_KV_HEADS = 4
NSA_HEAD_DIM = 64
NSA_REP = NSA_HEADS // NSA_KV_HEADS
NSA_WIDTH = NSA_HEADS * NSA_HEAD_DIM
NSA_KV_WIDTH = NSA_KV_HEADS * NSA_HEAD_DIM
CMP_BLOCK = 32
CMP_STRIDE = 16
CMP_HIDDEN = 256
SLC_BLOCK = 64
SLC_TOPK = 16
WINDOW = 512
NSA_Q_CHUNK = 32

REL_BUCKETS = 32
REL_MAX_DIST = 128

X_HEADS = 4
X_HEAD_DIM = 128
X_WIDTH = X_HEADS * X_HEAD_DIM

D_FF = 4 * D_MODEL

IN_PROJ_SIZES = (SSD_D_INNER, SSD_CONV_DIM, SSD_HEADS, NSA_WIDTH, 6 * NSA_KV_WIDTH, 3 * NSA_HEADS, 2 * D_MODEL)
IN_PROJ_DIM = SSD_D_INNER + SSD_CONV_DIM + SSD_HEADS + NSA_WIDTH + 6 * NSA_KV_WIDTH + 3 * NSA_HEADS + 2 * D_MODEL

kernel_name = "hybrid_ssd_nsa_gated_layer"


def rms_norm(x, g):
    xf = x.astype(jnp.float32)
    y = xf * lax.rsqrt(jnp.mean(xf * xf, axis=-1, keepdims=True) + NORM_EPS)
    return (y * g.astype(jnp.float32)).astype(x.dtype)


def t5_bucket(dist):
    n = jnp.maximum(dist, 0)
    max_exact = REL_BUCKETS // 2
    nf = jnp.maximum(n, 1).astype(jnp.float32)
    large = max_exact + (jnp.log(nf / max_exact) / math.log(REL_MAX_DIST / max_exact)
                         * (REL_BUCKETS - max_exact)).astype(jnp.int32)
    large = jnp.minimum(large, REL_BUCKETS - 1)
    return jnp.where(n < max_exact, n, large)


def causal_depthwise_conv(u, w, b):
    k = w.shape[0]
    out = lax.conv_general_dilated(
        u, w[:, None, :].astype(u.dtype), window_strides=(1,), padding=[(k - 1, 0)],
        dimension_numbers=("NWC", "WIO", "NWC"), feature_group_count=u.shape[-1])
    return out + b.astype(u.dtype)


def _segsum_from_cumsum(cs):
    t = cs.shape[-1]
    causal = jnp.tril(jnp.ones((t, t), dtype=bool))
    return jnp.where(causal, cs[..., :, None] - cs[..., None, :], -jnp.inf)


def ssd_chunked_scan(xh, dt, a_neg, bm, cm):
    b, s, h, p = xh.shape
    g, n = bm.shape[2], bm.shape[3]
    r = h // g
    nc, l = s // SSD_CHUNK, SSD_CHUNK
    xd = (xh * dt[..., None]).reshape(b, nc, l, g, r, p)
    a = (dt * a_neg).reshape(b, nc, l, g, r).transpose(0, 3, 4, 1, 2)
    bc = bm.reshape(b, nc, l, g, n)
    cc = cm.reshape(b, nc, l, g, n)
    a_cs = jnp.cumsum(a, axis=-1)
    decay_in = jnp.exp(_segsum_from_cumsum(a_cs))
    cb = jnp.einsum("bclgn,bcsgn->bgcls", cc, bc)
    y_diag = jnp.einsum("bgrcls,bcsgrp->bclgrp", cb[:, :, None] * decay_in, xd)
    decay_to_end = jnp.exp(a_cs[..., -1:] - a_cs).transpose(0, 3, 4, 1, 2)
    states = jnp.einsum("bclgn,bclgrp->bcgrpn", bc, xd * decay_to_end[..., None])
    chunk_cs = jnp.cumsum(jnp.pad(a_cs[..., -1], ((0, 0), (0, 0), (0, 0), (1, 0))), axis=-1)
    decay_chunk = jnp.exp(_segsum_from_cumsum(chunk_cs))
    states = jnp.concatenate([jnp.zeros_like(states[:, :1]), states], axis=1)
    carried = jnp.einsum("bgrzc,bcgrpn->bzgrpn", decay_chunk, states)[:, :-1]
    decay_out = jnp.exp(a_cs).transpose(0, 3, 4, 1, 2)
    y_off = jnp.einsum("bclgn,bcgrpn->bclgrp", cc, carried) * decay_out[..., None]
    return (y_diag + y_off).reshape(b, s, h, p)


def ssd_mixer(z, xbc, dt_raw, conv_w, conv_b, dt_bias, a_log, d_skip, norm_w):
    f32 = jnp.float32
    b, s, _ = z.shape
    xbc = jax.nn.silu(causal_depthwise_conv(xbc, conv_w, conv_b))
    xs, bm, cm = jnp.split(xbc, [SSD_D_INNER, SSD_D_INNER + SSD_GROUPS * SSD_STATE], axis=-1)
    dt = jax.nn.softplus(dt_raw.astype(f32) + dt_bias.astype(f32))
    a_neg = -jnp.exp(a_log.astype(f32))
    xh = xs.astype(f32).reshape(b, s, SSD_HEADS, SSD_HEAD_DIM)
    y = ssd_chunked_scan(xh, dt, a_neg,
                         bm.astype(f32).reshape(b, s, SSD_GROUPS, SSD_STATE),
                         cm.astype(f32).reshape(b, s, SSD_GROUPS, SSD_STATE))
    y = y + xh * d_skip.astype(f32)[:, None]
    y = y.reshape(b, s, SSD_D_INNER) * jax.nn.silu(z.astype(f32))
    yg = y.reshape(b, s, SSD_GROUPS, SSD_D_INNER // SSD_GROUPS)
    yg = yg * lax.rsqrt(jnp.mean(yg * yg, axis=-1, keepdims=True) + NORM_EPS)
    return (yg.reshape(b, s, SSD_D_INNER) * norm_w.astype(f32)).astype(z.dtype)


def nsa_mixer(q, kv, gate_logits, cmp_pos, cmp_w1, cmp_w2, rel_bias):
    f32 = jnp.float32
    b, s, _ = q.shape
    G, R, hd = NSA_KV_HEADS, NSA_REP, NSA_HEAD_DIM
    scale = hd ** -0.5
    qh = q.astype(f32).reshape(b, s, G, R, hd)
    kv = kv.astype(f32).reshape(b, s, 6, G, hd)
    k_cmp, v_cmp, k_slc, v_slc, k_win, v_win = [kv[:, :, i] for i in range(6)]
    table = rel_bias.astype(f32).reshape(REL_BUCKETS, G, R)

    n_cmp = (s - CMP_BLOCK) // CMP_STRIDE + 1
    blk_idx = CMP_STRIDE * jnp.arange(n_cmp)[:, None] + jnp.arange(CMP_BLOCK)[None, :]

    def compress(u, pos, w1, w2):
        ub = u[:, blk_idx] + pos.astype(f32)[:, None, :]
        ub = ub.transpose(0, 1, 3, 2, 4).reshape(b, n_cmp, G, CMP_BLOCK * hd)
        return jax.nn.silu(ub @ w1.astype(f32)) @ w2.astype(f32)

    kc = compress(k_cmp, cmp_pos[0], cmp_w1[0], cmp_w2[0])
    vc = compress(v_cmp, cmp_pos[1], cmp_w1[1], cmp_w2[1])
    cmp_start = CMP_STRIDE * jnp.arange(n_cmp)
    cmp_end = cmp_start + CMP_BLOCK - 1

    n_slc = s // SLC_BLOCK
    top_k = min(SLC_TOPK, n_slc)
    slc_start = SLC_BLOCK * jnp.arange(n_slc)
    overlap = ((cmp_start[:, None] <= slc_start[None, :] + SLC_BLOCK - 1)
               & (cmp_end[:, None] >= slc_start[None, :])).astype(f32)
    ks_blocks = k_slc.reshape(b, n_slc, SLC_BLOCK, G, hd).transpose(0, 3, 1, 2, 4)
    vs_blocks = v_slc.reshape(b, n_slc, SLC_BLOCK, G, hd).transpose(0, 3, 1, 2, 4)
    gather_blocks = jax.vmap(jax.vmap(lambda kb, ix: kb[ix]))
    bias_by_group = jax.vmap(lambda tbl, bk: tbl[bk], in_axes=(1, 1), out_axes=1)

    kw_pad = jnp.pad(k_win, ((0, 0), (WINDOW, 0), (0, 0), (0, 0)))
    vw_pad = jnp.pad(v_win, ((0, 0), (WINDOW, 0), (0, 0), (0, 0)))
    jb = jnp.arange(n_slc)

    def chunk(ci):
        t0 = ci * NSA_Q_CHUNK
        tq = t0 + jnp.arange(NSA_Q_CHUNK)
        qc = lax.dynamic_slice_in_dim(qh, t0, NSA_Q_CHUNK, axis=1) * scale

        lg = jnp.einsum("btgrd,bcgd->bgrtc", qc, kc)
        lg = lg + table[t5_bucket(tq[:, None] - cmp_end[None, :])].transpose(2, 3, 0, 1)
        valid = cmp_end[None, :] <= tq[:, None]
        p_cmp = jax.nn.softmax(jnp.where(valid, lg, NEG_INF), axis=-1) * valid
        o_cmp = jnp.einsum("bgrtc,bcgd->btgrd", p_cmp, vc)

        imp = jnp.einsum("bgrtc,cn->bgtn", p_cmp, overlap)
        tb = tq // SLC_BLOCK
        forced = (jb[None, :] == 0) | (jb[None, :] == tb[:, None]) | (jb[None, :] == tb[:, None] - 1)
        imp = jnp.where(forced, FORCE_SCORE, imp)
        imp = jnp.where(jb[None, :] > tb[:, None], NEG_INF, imp)
        sc, idx = lax.top_k(imp, top_k)
        sel_ok = sc > 0.5 * NEG_INF
        kg = gather_blocks(ks_blocks, idx)
        vg = gather_blocks(vs_blocks, idx)
        kpos = idx[..., None] * SLC_BLOCK + jnp.arange(SLC_BLOCK)
        dist = tq[None, None, :, None, None] - kpos
        m = sel_ok[..., None] & (dist >= 0)
        lg = jnp.einsum("btgrd,bgtkld->bgrtkl", qc, kg)
        lg = lg + jnp.moveaxis(bias_by_group(table, t5_bucket(dist)), -1, 2)
        lg = jnp.where(m[:, :, None], lg, NEG_INF)
        lg = lg.reshape(b, G, R, NSA_Q_CHUNK, top_k * SLC_BLOCK)
        p = jax.nn.softmax(lg, axis=-1).reshape(b, G, R, NSA_Q_CHUNK, top_k, SLC_BLOCK)
        o_slc = jnp.einsum("bgrtkl,bgtkld->btgrd", p, vg)

        kw = lax.dynamic_slice_in_dim(kw_pad, t0, WINDOW + NSA_Q_CHUNK, axis=1)
        vw = lax.dynamic_slice_in_dim(vw_pad, t0, WINDOW + NSA_Q_CHUNK, axis=1)
        kpos_w = t0 - WINDOW + jnp.arange(WINDOW + NSA_Q_CHUNK)
        dist_w = tq[:, None] - kpos_w[None, :]
        m_w = (kpos_w[None, :] >= 0) & (dist_w >= 0) & (dist_w < WINDOW)
        lg = jnp.einsum("btgrd,bkgd->bgrtk", qc, kw)
        lg = lg + table[t5_bucket(dist_w)].transpose(2, 3, 0, 1)
        p = jax.nn.softmax(jnp.where(m_w, lg, NEG_INF), axis=-1)
        o_win = jnp.einsum("bgrtk,bkgd->btgrd", p, vw)
        return o_cmp, o_slc, o_win

    o_cmp, o_slc, o_win = lax.map(chunk, jnp.arange(s // NSA_Q_CHUNK))

    def unchunk(o):
        return jnp.moveaxis(o, 0, 1).reshape(b, s, G, R, hd)

    gates = jax.nn.sigmoid(gate_logits.astype(f32)).reshape(b, s, 3, G, R)[..., None]
    o = gates[:, :, 0] * unchunk(o_cmp) + gates[:, :, 1] * unchunk(o_slc) + gates[:, :, 2] * unchunk(o_win)
    return o.reshape(b, s, NSA_WIDTH).astype(q.dtype)


def memory_cross_attention(h, m, w_q, w_kv, w_o):
    b, s, _ = h.shape
    ml = m.shape[1]
    q = (h @ w_q).reshape(b, s, X_HEADS, X_HEAD_DIM).astype(jnp.float32)
    kv = (m @ w_kv).reshape(b, ml, 2, X_HEADS, X_HEAD_DIM).astype(jnp.float32)
    lg = jnp.einsum("bshd,bmhd->bhsm", q, kv[:, :, 0]) * (X_HEAD_DIM ** -0.5)
    p = jax.nn.softmax(lg, axis=-1)
    o = jnp.einsum("bhsm,bmhd->bshd", p, kv[:, :, 1]).reshape(b, s, X_WIDTH).astype(h.dtype)
    return o @ w_o


def setup_inputs(seed: int = 0) -> dict:
    key = jax.random.key(seed)
    ks = jax.random.split(key, 32)
    f32 = jnp.float32

    def nrm(k, shape, scale):
        return jax.random.normal(k, shape, f32) * scale

    def gain(k, width=D_MODEL):
        return 1.0 + 0.05 * jax.random.normal(k, (DEPTH, width), f32)

    dt = jnp.exp(jax.random.uniform(ks[5], (DEPTH, SSD_HEADS), f32, math.log(1e-3), math.log(1e-1)))
    return {
        "x": nrm(ks[0], (BATCH, SEQ, D_MODEL), 1.0),
        "mem": nrm(ks[1], (BATCH, MEM_LEN, D_MODEL), 1.0),
        "w_in": nrm(ks[2], (DEPTH, D_MODEL, IN_PROJ_DIM), D_MODEL ** -0.5),
        "ssd_conv_w": nrm(ks[3], (DEPTH, SSD_CONV, SSD_CONV_DIM), SSD_CONV ** -0.5),
        "ssd_conv_b": nrm(ks[4], (DEPTH, SSD_CONV_DIM), 0.01),
        "ssd_dt_bias": dt + jnp.log(-jnp.expm1(-dt)),
        "ssd_a_log": jnp.log(jax.random.uniform(ks[6], (DEPTH, SSD_HEADS), f32, 1.0, 16.0)),
        "ssd_d_skip": 1.0 + 0.1 * jax.random.normal(ks[7], (DEPTH, SSD_HEADS), f32),
        "ssd_norm": gain(ks[8], SSD_D_INNER),
        "cmp_pos": nrm(ks[9], (DEPTH, 2, CMP_BLOCK, NSA_HEAD_DIM), 0.1),
        "cmp_w1": nrm(ks[10], (DEPTH, 2, CMP_BLOCK * NSA_HEAD_DIM, CMP_HIDDEN), (CMP_BLOCK * NSA_HEAD_DIM) ** -0.5),
        "cmp_w2": nrm(ks[11], (DEPTH, 2, CMP_HIDDEN, NSA_HEAD_DIM), CMP_HIDDEN ** -0.5),
        "rel_bias": nrm(ks[12], (REL_BUCKETS, NSA_HEADS), 0.5),
        "w_br_ssd": nrm(ks[13], (DEPTH, SSD_D_INNER, D_MODEL), SSD_D_INNER ** -0.5),
        "w_br_nsa": nrm(ks[14], (DEPTH, NSA_WIDTH, D_MODEL), NSA_WIDTH ** -0.5),
        "w_out": nrm(ks[15], (DEPTH, D_MODEL, D_MODEL), D_MODEL ** -0.5),
        "w_xq": nrm(ks[16], (DEPTH, D_MODEL, X_WIDTH), D_MODEL ** -0.5),
        "w_xkv": nrm(ks[17], (DEPTH, D_MODEL, 2 * X_WIDTH), D_MODEL ** -0.5),
        "w_xo": nrm(ks[18], (DEPTH, X_WIDTH, D_MODEL), X_WIDTH ** -0.5),
        "w_ff1": nrm(ks[19], (DEPTH, D_MODEL, D_FF), D_MODEL ** -0.5),
        "w_ff2": nrm(ks[20], (DEPTH, D_FF, D_MODEL), D_FF ** -0.5),
        "norm_mix_pre": gain(ks[21]),
        "norm_mix_post": gain(ks[22]),
        "norm_x_pre": gain(ks[23]),
        "norm_x_post": gain(ks[24]),
        "norm_mem": gain(ks[25]),
        "norm_ffn_pre": gain(ks[26]),
        "norm_ffn_post": gain(ks[27]),
    }


def reference(x, mem, w_in, ssd_conv_w, ssd_conv_b, ssd_dt_bias, ssd_a_log, ssd_d_skip, ssd_norm,
              cmp_pos, cmp_w1, cmp_w2, rel_bias, w_br_ssd, w_br_nsa, w_out, w_xq, w_xkv, w_xo,
              w_ff1, w_ff2, norm_mix_pre, norm_mix_post, norm_x_pre, norm_x_post, norm_mem,
              norm_ffn_pre, norm_ffn_post):
    split_points = np.cumsum(IN_PROJ_SIZES)[:-1].tolist()
    for l in range(DEPTH):
        h = rms_norm(x, norm_mix_pre[l])
        proj = h @ w_in[l]
        z, xbc, dt_raw, q, kv, nsa_gate, merge_gate = jnp.split(proj, split_points, axis=-1)
        y_ssd = ssd_mixer(z, xbc, dt_raw, ssd_conv_w[l], ssd_conv_b[l], ssd_dt_bias[l],
                          ssd_a_log[l], ssd_d_skip[l], ssd_norm[l])
        y_nsa = nsa_mixer(q, kv, nsa_gate, cmp_pos[l], cmp_w1[l], cmp_w2[l], rel_bias)
        g_ssd, g_nsa = jnp.split(jax.nn.sigmoid(merge_gate), 2, axis=-1)
        mixed = g_ssd * (y_ssd @ w_br_ssd[l]) + g_nsa * (y_nsa @ w_br_nsa[l])
        x = x + rms_norm(mixed @ w_out[l], norm_mix_post[l])
        h = rms_norm(x, norm_x_pre[l])
        m = rms_norm(mem, norm_mem[l])
        x = x + rms_norm(memory_cross_attention(h, m, w_xq[l], w_xkv[l], w_xo[l]), norm_x_post[l])
        h = rms_norm(x, norm_ffn_pre[l])
        ff = jnp.square(jax.nn.relu(h @ w_ff1[l])) @ w_ff2[l]
        x = x + rms_norm(ff, norm_ffn_post[l])
    return x
```

```python
import math
from contextlib import ExitStack
import numpy as np
import concourse.bass as bass
import concourse.mybir as mybir
from concourse.bass_utils import run_bass_kernel_spmd

F32 = mybir.dt.float32
BF16 = mybir.dt.bfloat16
AF = mybir.ActivationFunctionType
ALU = mybir.AluOpType


class Trk:
    __slots__ = ("w", "r")

    def __init__(self):
        self.w = {}
        self.r = {}


class Buf:
    def __init__(self, name, handle, space):
        self.name, self.h, self.space = name, handle, space
        self.trk = {None: Trk()}
        self.dsem = None

    def ap(self):
        return self.h.ap() if self.space == 'dram' else self.h[:]

    def __getitem__(self, idx):
        base = self.h.ap() if self.space == 'dram' else self.h
        return V(self, None, base[idx])


class V:
    __slots__ = ("buf", "key", "ap")

    def __init__(self, buf, key, ap):
        self.buf, self.key, self.ap = buf, key, ap

    def __getitem__(self, idx):
        return V(self.buf, self.key, self.ap[idx])

    def kk(self, key):
        return V(self.buf, key, self.ap)

    def re(self, pat, **kw):
        return V(self.buf, self.key, self.ap.rearrange(pat, **kw))

    def bc(self, shape):
        return V(self.buf, self.key, self.ap.to_broadcast(list(shape)))


WRITE_KW = ("out", "accum_out", "ap")


def _trks(v_):
    b = v_.buf
    if v_.key is None:
        return list(b.trk.values())
    if v_.key not in b.trk:
        b.trk[v_.key] = Trk()
    return [b.trk[v_.key], b.trk[None]]


def _merge(d, src):
    for k_, c in src.items():
        if d.get(k_, 0) < c:
            d[k_] = c


def _collect(v_, dw, dr):
    for t in _trks(v_):
        _merge(dw, t.w)
        if dr is not None:
            _merge(dr, t.r)


def _rec_read(v_, ev):
    t = v_.buf.trk.setdefault(v_.key, Trk())
    k_, c = ev
    if t.r.get(k_, 0) < c:
        t.r[k_] = c


def _rec_write(v_, ev):
    b = v_.buf
    k_, c = ev
    if v_.key is None:
        for kk in list(b.trk.keys()):
            if kk is not None:
                del b.trk[kk]
    t = b.trk.setdefault(v_.key, Trk())
    t.w = {k_: c}
    t.r = {}


class Eng:
    def __init__(self, fw, name, e, same_raw=True):
        self.fw, self.name, self.e = fw, name, e
        self.sem = fw.nc.alloc_semaphore("s_" + name)
        self.skey = (id(self.sem), self.sem)
        self.cnt = 0
        self.seen = {}
        self.same_raw = same_raw
        self.nwait = 0
        self.ninst = 0

    def wait_for(self, deps, skip_own):
        for (sid, sh), c in deps.items():
            if skip_own and sid == self.skey[0]:
                continue
            if self.seen.get(sid, 0) < c:
                self.e.wait_ge(sh, c)
                self.seen[sid] = c
                self.nwait += 1

    def __getattr__(self, fn):
        real = getattr(self.e, fn)

        def call(inc=True, **kw):
            reads, writes, raw = [], [], {}
            for k_, v_ in kw.items():
                if isinstance(v_, V):
                    (writes if k_ in WRITE_KW else reads).append(v_)
                    raw[k_] = v_.ap
                else:
                    raw[k_] = v_
            draw, doth = {}, {}
            for v_ in reads:
                _collect(v_, draw, None)
            for v_ in writes:
                _collect(v_, doth, doth)
            self.wait_for(draw, not self.same_raw)
            self.wait_for(doth, True)
            ins = real(**raw)
            self.ninst += 1
            if inc:
                self.cnt += 1
                ins.then_inc(self.sem, 1)
                ev = (self.skey, self.cnt)
            else:
                ev = (self.skey, self.cnt + 1)
            for v_ in reads:
                _rec_read(v_, ev)
            for v_ in writes:
                _rec_write(v_, ev)
            return ins

        return call


class FW:
    NDSEM = 56

    def __init__(self, nc):
        self.nc = nc
        self.pe = Eng(self, "pe", nc.tensor, same_raw=False)
        self.act = Eng(self, "act", nc.scalar)
        self.dve = Eng(self, "dve", nc.vector)
        self.pool = Eng(self, "pool", nc.gpsimd)
        self.sp = Eng(self, "sp", nc.sync)
        self.engs = [self.pe, self.act, self.dve, self.pool, self.sp]
        self.dpool = [[nc.alloc_semaphore("dq%d" % i), 0] for i in range(self.NDSEM)]
        self.dfree = list(range(self.NDSEM))
        self.ndma = 0
        self.uid = 0

    def dram(self, name, shape, dt=F32, kind="Internal"):
        h = self.nc.dram_tensor(name, list(shape), dt, kind=kind)
        return Buf(name, h, 'dram')

    def dma(self, out, in_, q="sp", **kw):
        eng = {"sp": self.sp, "act": self.act, "pool": self.pool}[q]
        owner = (out if out.buf.space != 'dram' else in_).buf
        assert owner.space != 'dram'
        if owner.dsem is None:
            owner.dsem = self.dfree.pop(0)
            owner.phase.dma_bufs.append(owner)
        ent = self.dpool[owner.dsem]
        draw, doth = {}, {}
        _collect(in_, draw, None)
        _collect(out, doth, doth)
        _merge(draw, doth)
        eng.wait_for(draw, False)
        ins = eng.e.dma_start(out=out.ap, in_=in_.ap, **kw)
        ent[1] += 16
        ins.then_inc(ent[0], 16)
        ev = ((id(ent[0]), ent[0]), ent[1])
        _rec_read(in_, ev)
        _rec_write(out, ev)
        self.ndma += 1
        return ins

    def barrier(self):
        alld = {}
        for e in self.engs:
            if e.cnt:
                alld[e.skey] = e.cnt
        for sh, c in self.dpool:
            if c:
                alld[(id(sh), sh)] = c
        for e in self.engs:
            e.wait_for(alld, True)

    def stats(self):
        return {e.name: (e.ninst, e.nwait) for e in self.engs} | {"dma": self.ndma}


class Phase:
    def __init__(self, f, tag):
        self.f, self.tag = f, tag
        self.es = ExitStack()
        self.dma_bufs = []

    def _mk(self, h, name, space):
        b = Buf(name, h, space)
        b.phase = self
        return b

    def sb(self, name, shape, dt=F32):
        self.f.uid += 1
        nm = "%s_%s_%d" % (self.tag, name, self.f.uid)
        h = self.es.enter_context(self.f.nc.sbuf_tensor(nm, list(shape), dt))
        return self._mk(h, nm, 'sb')

    def ps(self, name, shape=(128, 512), dt=F32):
        self.f.uid += 1
        nm = "%s_%s_%d" % (self.tag, name, self.f.uid)
        h = self.es.enter_context(self.f.nc.psum_tensor(nm, list(shape), dt))
        return self._mk(h, nm, 'ps')

    def close(self):
        self.f.barrier()
        for b in self.dma_bufs:
            self.f.dfree.append(b.dsem)
            b.dsem = None
        self.es.close()


def mm(f, out, pairs):
    n = len(pairs)
    for i, (l, r) in enumerate(pairs):
        f.pe.matmul(out=out, lhsT=l, rhs=r, start=(i == 0), stop=(i == n - 1), inc=(i == n - 1))


S = 4096
D = 1024
NT = 32
EPS = 1e-6
NEGM = -30000.0
OFF_Z, OFF_X, OFF_B, OFF_C, OFF_DT, OFF_Q, OFF_KV, OFF_NG, OFF_MG = 0, 2048, 4096, 5120, 6144, 6176, 7200, 8736, 8784
STAGES = ["inproj", "ssd", "nsa", "mix", "xattn", "ffn"]


def t5_bucket_np(dist):
    n = np.maximum(dist, 0)
    nf = np.maximum(n, 1).astype(np.float32)
    large = 16 + (np.log(nf / np.float32(16)) / np.float32(math.log(128 / 16)) * np.float32(16)).astype(np.int32)
    large = np.minimum(large, 31)
    return np.where(n < 16, n, large)


def host_consts():
    c = {}
    a = np.arange(128)
    mS = np.where(a[None, :] >= a[:, None], 0.0, NEGM).astype(np.float32)
    mW = np.where(a[None, :] < a[:, None], 0.0, NEGM).astype(np.float32)
    c["c_mask"] = np.stack([np.repeat(mS[:, None, :], 4, 1), np.repeat(mW[:, None, :], 4, 1)]).astype(np.float32)
    bp = np.arange(504) - 248
    dist = a[None, :] - 16 * bp[:, None] - 31
    c["c_cmask"] = np.where(dist >= 0, 0.0, NEGM).astype(np.float32)
    keep = np.ones((32, 128, 64), np.float32)
    add = np.zeros((32, 128, 64), np.float32)
    n = np.arange(64)
    for qt in range(32):
        tb = 2 * qt + (a >= 64).astype(np.int64)
        fut = n[None, :] > tb[:, None]
        f0 = np.broadcast_to(n[None, :] == 0, (128, 64))
        f1 = n[None, :] == (tb[:, None] - 1)
        f2 = n[None, :] == tb[:, None]
        keep[qt][fut | f0 | f1 | f2] = 0.0
        add[qt][fut] = -1e30
        add[qt][f0] = 1e9
        add[qt][f1] = 2e9
        add[qt][f2] = 3e9
    c["c_keep"] = keep
    c["c_add"] = add
    ex = np.zeros((64, 32, 128), np.float32)
    for kt in range(32):
        ex[2 * kt, kt, :64] = 1.0
        ex[2 * kt + 1, kt, 64:] = 1.0
    c["c_ex"] = ex
    cs = 16 * np.arange(256)
    ce = cs + 31
    ss = 64 * np.arange(64)
    ov = ((cs[:, None] <= ss[None, :] + 63) & (ce[:, None] >= ss[None, :])).astype(np.float32)
    ov[255] = 0.0
    c["c_ov"] = ov
    return c


def host_gather(rel_bias):
    a = np.arange(128)
    g = {}
    d0 = a[None, :] - a[:, None]
    tS = rel_bias[t5_bucket_np(d0)]
    tP = rel_bias[t5_bucket_np(d0 + 128)]
    tC = np.broadcast_to(rel_bias[31][None, None, :], (128, 128, 16))
    tb = np.stack([tS, tP, tC])
    g["g_tb"] = np.ascontiguousarray(tb.reshape(3, 128, 128, 4, 4).transpose(0, 3, 1, 4, 2)).astype(np.float32)
    bp = np.arange(504) - 248
    dist = a[None, :] - 16 * bp[:, None] - 31
    st = rel_bias[t5_bucket_np(dist)]
    g["g_cs"] = np.ascontiguousarray(st.transpose(2, 0, 1)).astype(np.float32)
    g["g_c31"] = np.ascontiguousarray(np.broadcast_to(rel_bias[31][None, :, None], (128, 16, 128))).astype(np.float32)
    return g


INPUT_NAMES = ["x", "mem", "w_in", "ssd_conv_w", "ssd_conv_b", "ssd_dt_bias", "ssd_a_log", "ssd_d_skip", "ssd_norm",
               "cmp_pos", "cmp_w1", "cmp_w2", "w_br_ssd", "w_br_nsa", "w_out", "w_xq", "w_xkv", "w_xo",
               "w_ff1", "w_ff2", "norm_mix_pre", "norm_mix_post", "norm_x_pre", "norm_x_post", "norm_mem",
               "norm_ffn_pre", "norm_ffn_post"]
SHAPES = {"x": [S, D], "mem": [256, D], "w_in": [D, 10832], "ssd_conv_w": [4, 4096], "ssd_conv_b": [1, 4096],
          "ssd_dt_bias": [1, 32], "ssd_a_log": [1, 32], "ssd_d_skip": [1, 32], "ssd_norm": [1, 2048],
          "cmp_pos": [2, 32, 64], "cmp_w1": [2, 2048, 256], "cmp_w2": [2, 256, 64], "w_br_ssd": [2048, D],
          "w_br_nsa": [D, D], "w_out": [D, D], "w_xq": [D, 512], "w_xkv": [D, 1024], "w_xo": [512, D],
          "w_ff1": [D, 4096], "w_ff2": [4096, D], "norm_mix_pre": [1, D], "norm_mix_post": [1, D],
          "norm_x_pre": [1, D], "norm_x_post": [1, D], "norm_mem": [1, D], "norm_ffn_pre": [1, D],
          "norm_ffn_post": [1, D],
          "c_mask": [2, 128, 4, 128], "c_cmask": [504, 128], "c_keep": [32, 128, 64], "c_add": [32, 128, 64],
          "c_ex": [64, 32, 128], "c_ov": [256, 64], "g_tb": [3, 4, 128, 4, 128], "g_cs": [16, 504, 128], "g_c31": [128, 16, 128]}


def make_consts(f, P):
    C = {}
    idf = P.sb("idf", [128, 128])
    f.pool.memset(ap=idf[:], constant=0.0)
    f.pool.affine_select(out=idf[:], in_=idf[:], pattern=[[-1, 128]], compare_op=ALU.not_equal, fill=1.0, base=0, channel_multiplier=1)
    idb = P.sb("idb", [128, 128], BF16)
    f.dve.tensor_copy(out=idb[:], in_=idf[:])
    U = P.sb("U", [128, 128])
    f.pool.memset(ap=U[:], constant=1.0)
    f.pool.affine_select(out=U[:], in_=U[:], pattern=[[1, 128]], compare_op=ALU.is_ge, fill=0.0, base=0, channel_multiplier=-1)
    Lm = P.sb("Lm", [128, 128])
    f.pool.memset(ap=Lm[:], constant=1.0)
    f.pool.affine_select(out=Lm[:], in_=Lm[:], pattern=[[-1, 128]], compare_op=ALU.is_ge, fill=0.0, base=-1, channel_multiplier=1)
    ones = P.sb("ones", [128, 128])
    f.pool.memset(ap=ones[:], constant=1.0)
    C.update(idf=idf, idb=idb, U=U, Lm=Lm, ones=ones)
    return C


def bc_load(f, P, name, dram_row, n):
    t = P.sb(name, [128, n])
    f.dma(t[:], dram_row.bc([128, n]))
    return t


def rms_rstd(f, ss_col, out_col, n):
    f.act.activation(out=out_col, in_=ss_col, func=AF.Sqrt, bias=EPS, scale=1.0 / n)
    f.dve.reciprocal(out=out_col, in_=out_col)


class Rms:
    def __init__(self, f, P, C, tag, pT):
        self.f, self.C, self.pT = f, C, pT
        self.ss = P.sb(tag + "ss", [128, 32])
        self.rs = P.sb(tag + "rs", [128, 32])
        self.xt = [P.sb(tag + "x%d" % i, [128, D]) for i in range(2)]
        self.junk = P.sb(tag + "junk", [128, D])
        self.hb = [P.sb(tag + "hb%d" % i, [128, D], BF16) for i in range(2)]

    def run(self, src, gbc, hT, tiles, col0):
        f, pT, xt, hb, ss, rs = self.f, self.pT, self.xt, self.hb, self.ss, self.rs
        n = len(tiles)
        f.dve.memset(ap=ss[:], constant=0.0)
        f.dma(xt[0][:], src[tiles[0] * 128:(tiles[0] + 1) * 128, :])
        for j, ti in enumerate(tiles):
            if j + 1 < n:
                f.dma(xt[(j + 1) % 2][:], src[tiles[j + 1] * 128:(tiles[j + 1] + 1) * 128, :])
            X = xt[j % 2]
            f.act.activation(out=self.junk[:], in_=X[:], func=AF.Square, accum_out=ss[:, j:j + 1])
            rms_rstd(f, ss[:, j:j + 1], rs[:, j:j + 1], D)
            f.dve.scalar_tensor_tensor(out=hb[j % 2][:], in0=X[:], scalar=rs[:, j:j + 1], in1=gbc[:], op0=ALU.mult, op1=ALU.mult)
            for k in range(8):
                f.pe.transpose(out=pT[j % 2][:, k, :], in_=hb[j % 2][:, k * 128:(k + 1) * 128], identity=self.C["idb"][:], inc=(k == 7))
            f.dve.tensor_copy(out=hT[:, :, col0 + j * 128:col0 + (j + 1) * 128], in_=pT[j % 2][:])


def wload(f, P, name, w, c0, wd, kch=8, r0=0):
    t = P.sb(name, [128, kch, wd], BF16)
    f.dma(t[:], w[r0:r0 + kch * 128, c0:c0 + wd].re("(k p) c -> p k c", p=128), q="pool")
    return t


def post_norm_residual(f, P, o_sb, xin, gbc, dst_rows, bufs):
    junk, ss, rs, res = bufs
    f.dve.memset(ap=ss[:], constant=0.0)
    f.act.activation(out=junk[:], in_=o_sb[:], func=AF.Square, accum_out=ss[:, 0:1])
    rms_rstd(f, ss[:, 0:1], rs[:, 0:1], D)
    f.dve.scalar_tensor_tensor(out=res[:], in0=o_sb[:], scalar=rs[:, 0:1], in1=gbc[:], op0=ALU.mult, op1=ALU.mult)
    f.pool.tensor_tensor(out=res[:], in0=res[:], in1=xin[:], op=ALU.add)
    f.dma(dst_rows, res[:])


def build(stage="ffn", dbg=()):
    nc = bass.Bass("TRN2", target_bir_lowering=False)
    f = FW(nc)
    I = {n: f.dram(n, SHAPES[n], F32, kind="ExternalInput") for n in SHAPES}
    out = f.dram("out", [S, D], F32, kind="ExternalOutput")
    dbg_out = {}
    st_i = STAGES.index(stage)

    def scr(name, shape, dt):
        return f.dram(name, shape, dt, kind=("ExternalOutput" if name in dbg else "Internal"))
    zs = scr("zs", [S, 2048], BF16)
    xs_tok = scr("xs_tok", [S, 2048], BF16)
    b_tok = scr("b_tok", [S, 1024], BF16)
    bcT = scr("bcT", [16, 128, S], BF16)
    qT = scr("qT", [16, 64, S], BF16)
    kT = scr("kT", [4, 4, 64, S], BF16)
    vtok = scr("vtok", [2, S, 260], BF16)
    gT = scr("gT", [16, 128, S], BF16)
    ysT = scr("ysT", [16, 128, S], BF16)
    ynT = scr("ynT", [8, 128, S], BF16)
    x1 = scr("x1", [S, D], F32)
    x2 = scr("x2", [S, D], F32)
    csb = scr("csb", [16, 504, 128], BF16)
    w2b = scr("w2b", [4096, D], BF16)

    PC = Phase(f, "c")
    C = make_consts(f, PC)
    dt_all = PC.sb("dt_all", [128, NT, 32])
    gate_all = PC.sb("gate_all", [128, NT, 48])

    PH = Phase(f, "h")
    hT = PH.sb("hT", [128, 8, S], BF16)
    psA = [PH.ps("psA%d" % i) for i in range(2)]
    psB = [PH.ps("psB%d" % i) for i in range(2)]
    psT = [PH.ps("psT%d" % i, [128, 8, 128], BF16) for i in range(2)]
    P = Phase(f, "a")
    gbc = bc_load(f, P, "gpre", I["norm_mix_pre"][0:1, :], D)
    Rms(f, P, C, "a", psT).run(I["x"], gbc, hT, list(range(NT)), 0)
    P.close()
    P = Phase(f, "b1")
    w_in = I["w_in"]

    wz = wload(f, P, "wz", w_in, OFF_Z, 2048)
    zst = [P.sb("zst%d" % i, [128, 2048], BF16) for i in range(2)]
    n = 0
    for i in range(NT):
        for j in range(4):
            ps = psA[n % 2]
            n += 1
            mm(f, ps[:], [(hT[:, k, i * 128:(i + 1) * 128], wz[:, k, j * 512:(j + 1) * 512]) for k in range(8)])
            f.act.activation(out=zst[i % 2][:, j * 512:(j + 1) * 512], in_=ps[:], func=AF.Silu)
        f.dma(zs[i * 128:(i + 1) * 128, :], zst[i % 2][:])

    P.close()
    P = Phase(f, "b2")
    cwT = P.sb("cwT", [128, 128])
    f.dma(cwT[:], I["ssd_conv_w"][:].re("k (j c) -> (k j) c", c=128))
    cbT = P.sb("cbT", [32, 128])
    f.dma(cbT[:], I["ssd_conv_b"][:].re("o (j c) -> (o j) c", c=128))
    pcw = psA[0][:, 0:160]
    f.pe.matmul(out=pcw[:, 0:128], lhsT=cwT[:], rhs=C["idf"][:], start=True, stop=True)
    f.pe.matmul(out=pcw[:, 128:160], lhsT=cbT[:], rhs=C["idf"][0:32, 0:32], start=True, stop=True)
    cw = P.sb("cw", [128, 5, 32])
    f.dve.tensor_copy(out=cw[:].re("p k j -> p (k j)"), in_=pcw)

    pre = [P.sb("pre%d" % i, [128, S + 3], BF16) for i in range(2)]
    for b_ in pre:
        f.dve.memset(ap=b_[:, 0:3], constant=0.0)
    xc = [P.sb("xc%d" % i, [128, S], BF16) for i in range(2)]
    dg = [P.sb("dg%d" % i, [128, 4, 128], BF16) for i in range(2)]
    tst = [P.sb("tst%d" % i, [128, NT, 128], BF16) for i in range(2)]
    wcb = [P.sb("wcb%d" % i, [128, 8, 128], BF16) for i in range(2)]
    for cc in range(32):
        f.dma(wcb[cc % 2][:], w_in[:, OFF_X + cc * 128:OFF_X + (cc + 1) * 128].re("(k p) c -> p k c", p=128), q="pool")
        W = wcb[cc % 2]
        pr, X, G = pre[cc % 2], xc[cc % 2], dg[cc % 2]
        for k in range(4):
            f.dve.tensor_scalar(out=G[:, k, :], in0=C["idf"][:], scalar1=cw[:, k, cc:cc + 1], scalar2=None, op0=ALU.mult)
        for tb in range(8):
            ps = psA[tb % 2]
            mm(f, ps[:], [(W[:, k, :], hT[:, k, tb * 512:(tb + 1) * 512]) for k in range(8)])
            f.dve.tensor_copy(out=pr[:, 3 + tb * 512:3 + (tb + 1) * 512], in_=ps[:])
        for tb in range(8):
            ps = psB[tb % 2]
            mm(f, ps[:], [(G[:, k, :], pr[:, tb * 512 + k:tb * 512 + k + 512]) for k in range(4)])
            f.act.activation(out=X[:, tb * 512:(tb + 1) * 512], in_=ps[:], func=AF.Silu, bias=cw[:, 4, cc:cc + 1])
        if cc >= 16:
            f.dma(bcT[cc - 16], X[:])
        if cc < 24:
            T = tst[cc % 2]
            for g4 in range(4):
                pt = psT[g4 % 2]
                for j in range(8):
                    i = g4 * 8 + j
                    f.pe.transpose(out=pt[:, j, :], in_=X[:, i * 128:(i + 1) * 128], identity=C["idb"][:], inc=(j == 7))
                f.dve.tensor_copy(out=T[:, g4 * 8:(g4 + 1) * 8, :], in_=pt[:])
            dst = xs_tok[:, cc * 128:(cc + 1) * 128] if cc < 16 else b_tok[:, (cc - 16) * 128:(cc - 15) * 128]
            for q4 in range(4):
                f.dma(dst[q4 * 1024:(q4 + 1) * 1024, :].re("(i p) c -> p i c", p=128), T[:, q4 * 8:(q4 + 1) * 8, :])

    P.close()
    P = Phase(f, "b3")
    wdt = wload(f, P, "wdt", w_in, OFF_DT, 32)
    wng = wload(f, P, "wng", w_in, OFF_NG, 48)
    dtb = bc_load(f, P, "dtb", I["ssd_dt_bias"][0:1, :], 32)
    for i in range(NT):
        ps = psA[i % 2]
        mm(f, ps[:, 0:32], [(hT[:, k, i * 128:(i + 1) * 128], wdt[:, k, :]) for k in range(8)])
        mm(f, ps[:, 64:112], [(hT[:, k, i * 128:(i + 1) * 128], wng[:, k, :]) for k in range(8)])
        f.dve.tensor_tensor(out=dt_all[:, i, :], in0=ps[:, 0:32], in1=dtb[:], op=ALU.add)
        f.dve.tensor_copy(out=gate_all[:, i, :], in_=ps[:, 64:112])
    f.act.activation(out=dt_all[:], in_=dt_all[:], func=AF.Exp)
    f.act.activation(out=dt_all[:], in_=dt_all[:], func=AF.Ln, bias=1.0)
    f.act.activation(out=gate_all[:], in_=gate_all[:], func=AF.Sigmoid)

    fst = [P.sb("fst%d" % i, [128, S], BF16) for i in range(2)]
    jobs = [(OFF_Q + hp * 128, qT[2 * hp:2 * hp + 2], 0.125) for hp in range(8)]
    for ty, ko in enumerate((0, 256, 512, 1024)):
        for gp in range(2):
            jobs.append((OFF_KV + ko + gp * 128, kT[ty, 2 * gp:2 * gp + 2], 1.0))
    wq2 = [P.sb("wq%d" % i, [128, 8, 128], BF16) for i in range(2)]
    for ji, (c0, dst, sc) in enumerate(jobs):
        W = wq2[ji % 2]
        f.dma(W[:], w_in[:, c0:c0 + 128].re("(k p) c -> p k c", p=128), q="pool")
        Fs = fst[ji % 2]
        for tb in range(8):
            ps = psA[tb % 2]
            mm(f, ps[:], [(W[:, k, :], hT[:, k, tb * 512:(tb + 1) * 512]) for k in range(8)])
            f.act.activation(out=Fs[:, tb * 512:(tb + 1) * 512], in_=ps[:], func=AF.Copy, scale=sc)
        f.dma(dst.re("h d t -> (h d) t"), Fs[:])

    vst = [P.sb("vst%d" % i, [128, 4, 65], BF16) for i in range(2)]
    for v_ in vst:
        f.dve.memset(ap=v_[:], constant=1.0)
    for ty, ko in enumerate((768, 1280)):
        wv = wload(f, P, "wv%d" % ty, w_in, OFF_KV + ko, 256)
        for i in range(NT):
            ps = psA[i % 2]
            mm(f, ps[:, 0:256], [(hT[:, k, i * 128:(i + 1) * 128], wv[:, k, :]) for k in range(8)])
            f.dve.tensor_copy(out=vst[i % 2][:, :, 0:64], in_=ps[:, 0:256].re("p (g d) -> p g d", g=4))
            f.dma(vtok[ty, i * 128:(i + 1) * 128, :], vst[i % 2][:].re("p g d -> p (g d)"))

    xc = [P.sb("xg%d" % i, [128, S], BF16) for i in range(2)]
    wcb = [P.sb("wgb%d" % i, [128, 8, 128], BF16) for i in range(2)]
    for cc in range(16):
        W = wcb[cc % 2]
        f.dma(W[:], w_in[:, OFF_MG + cc * 128:OFF_MG + (cc + 1) * 128].re("(k p) c -> p k c", p=128), q="pool")
        X = xc[cc % 2]
        for tb in range(8):
            ps = psA[tb % 2]
            mm(f, ps[:], [(W[:, k, :], hT[:, k, tb * 512:(tb + 1) * 512]) for k in range(8)])
            f.act.activation(out=X[:, tb * 512:(tb + 1) * 512], in_=ps[:], func=AF.Sigmoid)
        f.dma(gT[cc], X[:])
    P.close()
    PH.close()

    if st_i == 0:
        f.barrier()
        return nc
    P = Phase(f, "ssd")
    ssd_phase(f, P, C, I, dt_all, zs, xs_tok, b_tok, bcT, ysT)
    P.close()
    def done():
        f.barrier()
        print(f.stats())
        return nc
    if st_i == 1:
        return done()
    P = Phase(f, "nsa")
    nsa_phase(f, P, C, I, gate_all, qT, kT, vtok, csb, ynT)
    P.close()
    if st_i == 2:
        return done()
    P = Phase(f, "mix")
    mix_phase(f, P, C, I, ysT, ynT, gT, x1)
    P.close()
    if st_i == 3:
        return done()
    P = Phase(f, "xat")
    xattn_phase(f, P, C, I, x1, x2)
    P.close()
    if st_i == 4:
        return done()
    P = Phase(f, "ffn")
    ffn_phase(f, P, C, I, x2, out, w2b)
    P.close()
    return done()


def ssd_phase(f, P, C, I, dt_all, zs, xs_tok, b_tok, bcT, ysT):
    U, Lm, ones, idb = C["U"], C["Lm"], C["ones"], C["idb"]
    aneg = bc_load(f, P, "aneg", I["ssd_a_log"][0:1, :], 32)
    f.act.activation(out=aneg[:], in_=aneg[:], func=AF.Exp)
    f.dve.tensor_scalar(out=aneg[:], in0=aneg[:], scalar1=-1.0, scalar2=None, op0=ALU.mult)
    dsk = bc_load(f, P, "dsk", I["ssd_d_skip"][0:1, :], 32)
    nw = bc_load(f, P, "nw", I["ssd_norm"][0:1, :], 2048)
    Sst = P.sb("Sst", [128, 8, 256])
    f.dve.memset(ap=Sst[:], constant=0.0)
    Sbf = P.sb("Sbf", [128, 8, 256], BF16)
    f.dve.memset(ap=Sbf[:], constant=0.0)
    NB = 4
    Dd = P.sb("Dd", [128, 32, 128], BF16)
    for h in range(32):
        f.dve.tensor_scalar(out=Dd[:, h, :], in0=C["idf"][:], scalar1=dsk[:, h:h + 1], scalar2=None, op0=ALU.mult)
    xs3 = [P.sb("xs%d" % i, [128, 32, 64], BF16) for i in range(NB)]
    bt3 = [P.sb("bt%d" % i, [128, 1024], BF16) for i in range(NB)]
    BC3 = [P.sb("BC%d" % i, [128, 16, 128], BF16) for i in range(NB)]
    z3 = [P.sb("z%d" % i, [128, 2048], BF16) for i in range(NB)]
    a2 = [P.sb("a_sb%d" % i, [128, 32]) for i in range(2)]
    acs2 = [P.sb("acs%d" % i, [128, 32]) for i in range(2)]
    ea2 = [P.sb("ea%d" % i, [128, 32]) for i in range(2)]
    dte2 = [P.sb("dte%d" % i, [128, 32]) for i in range(2)]
    dtd2 = [P.sb("dtd%d" % i, [128, 32]) for i in range(2)]
    dA2 = [P.sb("dA%d" % i, [128, 32]) for i in range(2)]
    xd2 = [P.sb("xd%d" % i, [128, 32, 64], BF16) for i in range(2)]
    xdd2 = [P.sb("xdd%d" % i, [128, 32, 64], BF16) for i in range(2)]
    La2 = [P.sb("La%d" % i, [128, 4, 128]) for i in range(2)]
    cbm = [P.sb("cbm%d" % i, [128, 128]) for i in range(2)]
    E = [P.sb("E%d" % i, [128, 4, 128]) for i in range(2)]
    MT = [P.sb("MT%d" % i, [128, 4, 128], BF16) for i in range(2)]
    y2 = [P.sb("y%d" % i, [128, 2048]) for i in range(2)]
    tmp = P.sb("tmp", [128, 2048])
    yo = P.sb("yo", [128, 256])
    gss = P.sb("gss", [128, 8])
    grs = P.sb("grs", [128, 8])
    junk = P.sb("junk", [128, 256])
    yb = P.sb("yb", [128, 2048], BF16)
    yT = [P.sb("yT%d" % i, [128, 16, 128], BF16) for i in range(2)]
    ps_s = P.ps("ps_s")
    ps_cb = P.ps("ps_cb")
    ps_D = [P.ps("ps_D%d" % i) for i in range(2)]
    ps_y = P.ps("ps_y")
    ps_o = P.ps("ps_o")
    ps_st = P.ps("ps_st")
    ps_T = P.ps("ps_T", [128, 8, 128], BF16)

    def loads(c):
        b = c % NB
        r = slice(c * 128, (c + 1) * 128)
        f.dma(xs3[b][:].re("p h d -> p (h d)"), xs_tok[r, :])
        f.dma(bt3[b][:], b_tok[r, :])
        f.dma(BC3[b][:], bcT[:, :, r].re("j n l -> n j l"))
        f.dma(z3[b][:], zs[r, :])

    def pro(c):
        p = c % 2
        xs = xs3[c % NB]
        dt = dt_all[:, c, :]
        a_sb, acs, ea, dte, dtd, dA = a2[p], acs2[p], ea2[p], dte2[p], dtd2[p], dA2[p]
        f.dve.tensor_tensor(out=a_sb[:], in0=dt, in1=aneg[:], op=ALU.mult)
        f.pe.matmul(out=ps_s[:, 0:32], lhsT=U[:], rhs=a_sb[:], start=True, stop=True)
        f.pe.matmul(out=ps_s[:, 64:96], lhsT=ones[:], rhs=a_sb[:], start=True, stop=True)
        f.act.activation(out=acs[:], in_=ps_s[:, 0:32], func=AF.Copy)
        f.act.activation(out=ea[:], in_=ps_s[:, 0:32], func=AF.Exp)
        f.act.activation(out=dA[:], in_=ps_s[:, 64:96], func=AF.Exp)
        f.dve.tensor_tensor(out=dte[:], in0=ps_s[:, 64:96], in1=acs[:], op=ALU.subtract)
        f.act.activation(out=dte[:], in_=dte[:], func=AF.Exp)
        f.dve.tensor_tensor(out=dtd[:], in0=dte[:], in1=dt, op=ALU.mult)
        f.dve.tensor_tensor(out=xd2[p][:], in0=xs[:], in1=dt.re("p (h o) -> p h o", o=1).bc([128, 32, 64]), op=ALU.mult)
        f.pool.tensor_tensor(out=xdd2[p][:], in0=xs[:], in1=dtd[:].re("p (h o) -> p h o", o=1).bc([128, 32, 64]), op=ALU.mult)

    def mkLa(c, g):
        a_sb = a2[c % 2]
        f.pool.tensor_tensor(out=La2[g % 2][:], in0=Lm[:].re("p (o j) -> p o j", o=1).bc([128, 4, 128]),
                             in1=a_sb[:, 4 * g:4 * g + 4].re("p (h o) -> p h o", o=1).bc([128, 4, 128]), op=ALU.mult)

    def stA(c, g):
        gb = g % 2
        BC = BC3[c % NB]
        mm(f, ps_cb[:, 0:128], [(BC[:, g, :], BC[:, 8 + g, :])])
        f.dve.tensor_tensor(out=cbm[gb][:], in0=ps_cb[:, 0:128], in1=U[:], op=ALU.mult)
        pD = ps_D[gb]
        for r in range(4):
            f.pe.matmul(out=pD[:, r * 128:(r + 1) * 128], lhsT=La2[gb][:, r, :], rhs=U[:], start=True, stop=True, inc=(r == 3))
        f.act.activation(out=E[gb][:].re("p r i -> p (r i)"), in_=pD[:], func=AF.Exp)
        f.dve.tensor_tensor(out=MT[gb][:], in0=E[gb][:], in1=cbm[gb][:].re("p (o i) -> p o i", o=1).bc([128, 4, 128]), op=ALU.mult)

    def stB(c, g):
        gb = g % 2
        p = c % 2
        BC, bt = BC3[c % NB], bt3[c % NB]
        y, ea, dA, xd, xdd = y2[p], ea2[p], dA2[p], xd2[p], xdd2[p]
        xs = xs3[c % NB]
        for r in range(4):
            f.pe.matmul(out=ps_y[:, r * 64:(r + 1) * 64], lhsT=MT[gb][:, r, :], rhs=xd[:, 4 * g + r, :], start=True, stop=False, inc=False)
            f.pe.matmul(out=ps_y[:, r * 64:(r + 1) * 64], lhsT=Dd[:, 4 * g + r, :], rhs=xs[:, 4 * g + r, :], start=False, stop=True, inc=(r == 3))
        mm(f, ps_o[:, 0:256], [(BC[:, 8 + g, :], Sbf[:, g, :].kk(g))])
        mm(f, ps_st[:, 0:256], [(bt[:, g * 128:(g + 1) * 128], xdd[:, 4 * g:4 * g + 4, :].re("p r d -> p (r d)"))])
        f.dve.tensor_tensor(out=yo[:].re("p (r d) -> p r d", r=4), in0=ps_o[:, 0:256].re("p (r d) -> p r d", r=4),
                            in1=ea[:, 4 * g:4 * g + 4].re("p (r o) -> p r o", o=1).bc([128, 4, 64]), op=ALU.mult)
        f.dve.tensor_tensor(out=y[:, g * 256:(g + 1) * 256].kk(g), in0=yo[:], in1=ps_y[:, 0:256], op=ALU.add)
        f.pool.tensor_tensor(out=Sst[:, g, :].re("p (r d) -> p r d", r=4).kk(g), in0=Sst[:, g, :].re("p (r d) -> p r d", r=4).kk(g),
                             in1=dA[:, 4 * g:4 * g + 4].re("p (r o) -> p r o", o=1).bc([128, 4, 64]), op=ALU.mult)
        f.dve.tensor_tensor(out=Sst[:, g, :].kk(g), in0=Sst[:, g, :].kk(g), in1=ps_st[:, 0:256], op=ALU.add)
        f.act.activation(out=Sbf[:, g, :].kk(g), in_=Sst[:, g, :].kk(g), func=AF.Copy)

    def epi_ops(c):
        y, xs, zz = y2[c % 2], xs3[c % NB], z3[c % NB]
        T = yT[c % 2]
        ops = []
        ops.append(lambda: f.dve.tensor_tensor(out=y[:], in0=y[:], in1=zz[:], op=ALU.mult))

        def sq():
            f.dve.memset(ap=gss[:], constant=0.0)
            for g in range(8):
                f.act.activation(out=junk[:], in_=y[:, g * 256:(g + 1) * 256], func=AF.Square, accum_out=gss[:, g:g + 1])
            f.act.activation(out=grs[:], in_=gss[:], func=AF.Ln, bias=EPS, scale=1.0 / 256)
            f.act.activation(out=grs[:], in_=grs[:], func=AF.Exp, scale=-0.5)
        ops.append(sq)
        ops.append(lambda: f.pool.tensor_tensor(out=tmp[:].re("p (g d) -> p g d", g=8), in0=y[:].re("p (g d) -> p g d", g=8),
                                                in1=grs[:].re("p (g o) -> p g o", o=1).bc([128, 8, 256]), op=ALU.mult))
        ops.append(lambda: f.dve.tensor_tensor(out=yb[:], in0=tmp[:], in1=nw[:], op=ALU.mult))

        def tr(h2):
            for j in range(8):
                f.pe.transpose(out=ps_T[:, j, :], in_=yb[:, (h2 * 8 + j) * 128:(h2 * 8 + j + 1) * 128], identity=idb[:], inc=(j == 7))
            f.act.activation(out=T[:, h2 * 8:(h2 + 1) * 8, :], in_=ps_T[:], func=AF.Copy)
        ops.append(lambda: tr(0))
        ops.append(lambda: (tr(1), f.dma(ysT[:, :, c * 128:(c + 1) * 128].re("j f t -> f j t"), T[:])))
        return ops

    loads(0)
    loads(1)
    pro(0)
    deferred = []
    for c in range(NT):
        if c + 2 < NT:
            loads(c + 2)
        if c + 1 < NT:
            pro(c + 1)
        mkLa(c, 0)
        for g in range(8):
            if g + 1 < 8:
                mkLa(c, g + 1)
            stA(c, g)
            if g > 0:
                stB(c, g - 1)
            if deferred:
                deferred.pop(0)()
        stB(c, 7)
        while deferred:
            deferred.pop(0)()
        deferred = epi_ops(c)
    while deferred:
        deferred.pop(0)()


def nsa_phase(f, P, C, I, gate_all, qT, kT, vtok, csb, ynT):
    idb, idf = C["idb"], C["idf"]
    psS = [P.ps("psS%d" % i) for i in range(3)]
    po4 = P.ps("po4", [128, 4, 512])
    pso = [po4, po4, po4]
    psiB = P.ps("psiB")
    psi = psiB[:, 0:256].re("p (r n) -> p r n", r=4)
    selT = V(psiB, None, psiB.ap()[:, 256:320].bitcast(BF16))
    psT = V(psS[0], None, psS[0].ap().bitcast(BF16)).re("p (k t) -> p k t", k=8)
    tb = P.sb("tb", [128, 2, 4, 512], BF16)
    w4 = P.sb("w4", [128, 4, 512], BF16)
    ovb = P.sb("ovb", [128, 2, 64], BF16)
    keep = P.sb("keep", [128, 32, 64])
    addm = P.sb("addm", [128, 32, 64])
    kslc = P.sb("kslc", [128, 4, S], BF16)
    kwin = P.sb("kwin", [128, 4, S], BF16)
    c31f = P.sb("c31f", [128, 16, 128])
    vs1 = P.sb("vs1", [128, 32, 260], BF16)
    vw1 = P.sb("vw1", [128, 32, 260], BF16)
    kc = P.sb("kc", [128, 4, 256], BF16)
    vcR = P.sb("vcR", [128, 4, 2, 65], BF16)
    P2 = Phase(f, "nsas")
    m0 = P2.sb("m0", [128, 512])
    m1 = P2.sb("m1", [128, 512])
    f.dma(m0[:], I["c_mask"][0].re("k r t -> k (r t)"))
    f.dma(m1[:], I["c_mask"][1].re("k r t -> k (r t)"))
    stg = [P2.sb("stg%d" % i, [128, 512]) for i in range(3)]
    for g in range(4):
        for kind in range(3):
            f.dma(stg[kind][:], I["g_tb"][kind, g].re("k r t -> k (r t)"))
        f.dve.tensor_tensor(out=stg[0][:], in0=stg[0][:], in1=stg[2][:], op=ALU.subtract)
        f.dve.tensor_tensor(out=tb[:, 0, g, :], in0=stg[0][:], in1=m0[:], op=ALU.add)
        f.dve.tensor_tensor(out=tb[:, 1, g, :], in0=stg[1][:], in1=stg[2][:], op=ALU.subtract)
        f.dve.tensor_copy(out=w4[:, g, :], in_=m1[:])
    cmk = P2.sb("cmk", [126, 4, 128])
    f.dma(cmk[:], I["c_cmask"][:].re("(j p) a -> p j a", p=126))
    cst = [P2.sb("cst%d" % i, [126, 4, 128]) for i in range(2)]
    csb_ = [P2.sb("csb%d" % i, [126, 4, 128], BF16) for i in range(2)]
    for h in range(16):
        f.dma(cst[h % 2][:], I["g_cs"][h].re("(j p) a -> p j a", p=126))
        f.dve.tensor_tensor(out=csb_[h % 2][:], in0=cst[h % 2][:], in1=cmk[:], op=ALU.add)
        f.dma(csb[h].re("(j p) a -> p j a", p=126), csb_[h % 2][:])
    f.dma(ovb[:], I["c_ov"][:].re("(cc c) n -> c cc n", c=128), q="pool")
    f.dma(keep[:], I["c_keep"][:].re("q p n -> p q n"))
    f.dma(addm[:], I["c_add"][:].re("q p n -> p q n"))
    f.dma(kslc[0:64].kk("k"), kT[2].re("g d t -> d g t"))
    f.dma(kwin[0:64].kk("k"), kT[3].re("g d t -> d g t"))
    for g in range(4):
        f.dma(kslc[64:128, g, :].kk(("e", g)), I["c_ex"][:].re("n kt k -> n (kt k)"), q="pool")
        f.dma(kwin[64:128, g, :].kk(("e", g)), I["c_ex"][:].re("n kt k -> n (kt k)"), q="pool")
    f.dma(c31f[64:128], I["g_c31"][64:128])
    f.dve.tensor_scalar(out=c31f[64:128], in0=c31f[64:128], scalar1=NEGM, scalar2=None, op0=ALU.add)
    for q4 in range(4):
        f.dma(vs1[:, q4 * 8:(q4 + 1) * 8, :].kk(q4), vtok[0, q4 * 1024:(q4 + 1) * 1024, :].re("(kt p) c -> p kt c", p=128))
        f.dma(vw1[:, q4 * 8:(q4 + 1) * 8, :].kk(q4), vtok[1, q4 * 1024:(q4 + 1) * 1024, :].re("(kt p) c -> p kt c", p=128))
    f.dve.memset(ap=kc[:], constant=0.0)
    f.dve.memset(ap=vcR[:], constant=1.0)
    w1 = P2.sb("w1", [64, 32, 256], BF16)
    w2 = P2.sb("w2", [128, 2, 64], BF16)
    posr = P2.sb("posr", [32, 64])
    posT = P2.sb("posT", [64, 32], BF16)
    cstc = P2.sb("cstc", [128, 2])
    u = [P2.sb("u%d" % i, [64, S], BF16) for i in range(2)]
    hid = [P2.sb("hid%d" % i, [128, 2, 256], BF16) for i in range(2)]
    for h_ in hid:
        f.dve.memset(ap=h_[:], constant=0.0)
    n = 0
    for kv in range(2):
        f.dma(w1[:], I["cmp_w1"][kv].re("(l d) j -> d l j", d=64), q="pool")
        f.dma(w2[:], I["cmp_w2"][kv].re("(jh p) d -> p jh d", p=128), q="pool")
        f.dma(posr[:], I["cmp_pos"][kv])
        f.pe.matmul(out=psS[0][0:64, 0:32], lhsT=posr[:], rhs=idf[0:32, 0:32], start=True, stop=True)
        f.dve.tensor_copy(out=posT[:], in_=psS[0][0:64, 0:32])
        for jh in range(2):
            mm(f, psS[1][:, jh:jh + 1], [(w1[:, l, jh * 128:(jh + 1) * 128], posT[:, l:l + 1]) for l in range(32)])
        f.dve.tensor_copy(out=cstc[:], in_=psS[1][:, 0:2])
        for g in range(4):
            U_ = u[n % 2]
            H = hid[n % 2]
            n += 1
            f.dma(U_[:], kT[kv, g])
            uv = U_[:].re("d (c s) -> d c s", s=16)
            for jh in range(2):
                ps = psS[jh]
                mm(f, ps[:, 0:255], [(w1[:, l, jh * 128:(jh + 1) * 128],
                                       uv[:, (l // 16):(l // 16) + 255, l % 16]) for l in range(32)])
                f.act.activation(out=H[:, jh, 0:255], in_=ps[:, 0:255], func=AF.Silu, bias=cstc[:, jh:jh + 1])
            if kv == 0:
                mm(f, psS[2][0:64, 0:255], [(w2[:, jh, :], H[:, jh, 0:255]) for jh in range(2)])
                f.dve.tensor_copy(out=kc[0:64, g, 0:255], in_=psS[2][0:64, 0:255])
            else:
                for cc in range(2):
                    mm(f, psS[2][:, cc * 64:(cc + 1) * 64], [(H[:, jh, cc * 128:(cc + 1) * 128], w2[:, jh, :]) for jh in range(2)])
                f.dve.tensor_copy(out=vcR[:, g, :, 0:64], in_=psS[2][:, 0:128].re("p (cc d) -> p cc d", cc=2))
    P2.close()
    qS = [P.sb("qS%d" % i, [128, 16, 128], BF16) for i in range(2)]
    qW = [P.sb("qW%d" % i, [128, 16, 128], BF16) for i in range(2)]
    for i in range(2):
        f.dma(qS[i][64:128], I["g_c31"][64:128], q="pool")
        f.dma(qW[i][64:128], I["g_c31"][64:128], q="pool")
    tmpm = P.sb("tmpm", [128, 4, 128])
    cbt = [P.sb("cbt%d" % i, [128, 4, 2, 512], BF16) for i in range(2)]
    Eb = [P.sb("E%d" % i, [128, 4, 128], BF16) for i in range(8)]
    acc = P.sb("acc", [128, 16, 64])
    tmpo = P.sb("tmpo", [128, 4, 64])
    den = P.sb("den", [128, 4])
    wgt = P.sb("wgt", [128, 4])
    impt = P.sb("impt", [128, 4, 64])
    imp = P.sb("imp", [128, 64])
    rep = P.sb("rep", [128, 64])
    m8 = P.sb("m8", [128, 8])
    thr = P.sb("thr", [128, 1])
    selbs = [P.sb("selb%d" % i, [128, 128], BF16) for i in range(4)]
    for sb_ in selbs:
        f.dve.memset(ap=sb_[:], constant=0.0)
    yb = P.sb("yb", [128, 1024], BF16)
    ysb = [P.sb("ysb%d" % i, [128, 8, 128], BF16) for i in range(2)]
    ne = [0]
    ns = [0]

    def nextE():
        ne[0] += 1
        return Eb[ne[0] % 8]

    def nextS():
        ns[0] += 1
        return psS[ns[0] % 3]

    def qloads(qt):
        b = qt % 2
        f.dma(qS[b][0:64], qT[:, :, qt * 128:(qt + 1) * 128].re("h d t -> d h t"))
        f.dma(qW[b][0:64], qT[:, :, qt * 128:(qt + 1) * 128].re("h d t -> d h t"))
        for g in range(4):
            for cc in range(2):
                st = 248 - 8 * qt + 128 * cc
                f.dma(cbt[b][:, g, cc, :].re("c (r t) -> c r t", r=4).kk((g, cc)),
                      csb[4 * g:4 * g + 4, st:st + 128, :].re("r c t -> c r t"))

    pcp = [P.sb("pcp%d" % i, [128, 4, 65]) for i in range(2)]
    ncp = [0]

    def combine(po, br, g, first):
        ncp[0] += 1
        cp = pcp[ncp[0] % 2]
        f.act.activation(out=cp[:], in_=po[:, :, 0:65], func=AF.Copy)
        gt = gate_all[:, qt_cur[0], br * 16 + 4 * g:br * 16 + 4 * g + 4]
        if br == 0:
            f.dve.tensor_scalar(out=den[:], in0=cp[:, :, 64], scalar1=1e-30, scalar2=None, op0=ALU.max)
            f.dve.reciprocal(out=den[:], in_=den[:])
            f.dve.tensor_tensor(out=wgt[:], in0=den[:], in1=gt, op=ALU.mult)
        else:
            f.dve.reciprocal(out=den[:], in_=cp[:, :, 64])
            f.dve.tensor_tensor(out=wgt[:], in0=den[:], in1=gt, op=ALU.mult)
        wb = wgt[:].re("p (r o) -> p r o", o=1).bc([128, 4, 64])
        if first:
            f.dve.tensor_tensor(out=acc[:, 4 * g:4 * g + 4, :].kk(g), in0=cp[:, :, 0:64], in1=wb, op=ALU.mult)
        else:
            f.dve.tensor_tensor(out=tmpo[:], in0=cp[:, :, 0:64], in1=wb, op=ALU.mult)
            f.dve.tensor_tensor(out=acc[:, 4 * g:4 * g + 4, :].kk(g), in0=acc[:, 4 * g:4 * g + 4, :].kk(g), in1=tmpo[:], op=ALU.add)

    qt_cur = [0]
    pend = []

    def push(fn):
        pend.append(fn)
        while len(pend) > 2:
            pend.pop(0)()

    def flush():
        while pend:
            pend.pop(0)()

    def qk_exp(lhsT, rhs_q, extras):
        ps = nextS()
        ne_ = len(extras)
        f.pe.matmul(out=ps[:], lhsT=lhsT, rhs=rhs_q, start=True, stop=(ne_ == 0), inc=(ne_ == 0))
        for i, (l_, r_) in enumerate(extras):
            f.pe.matmul(out=ps[:], lhsT=l_, rhs=r_, start=False, stop=(i == ne_ - 1), inc=(i == ne_ - 1))
        E_ = nextE()
        f.act.activation(out=E_[:].re("p r t -> p (r t)"), in_=ps[:], func=AF.Exp)
        return E_

    def topk_mask(qt, g):
        f.dve.tensor_tensor(out=impt[:], in0=psi[:, :, 0:64], in1=den[:].re("p (r o) -> p r o", o=1).bc([128, 4, 64]), op=ALU.mult)
        f.dve.tensor_tensor(out=imp[:], in0=impt[:, 0, :], in1=impt[:, 1, :], op=ALU.add)
        f.dve.tensor_tensor(out=rep[:], in0=impt[:, 2, :], in1=impt[:, 3, :], op=ALU.add)
        f.dve.tensor_tensor(out=imp[:], in0=imp[:], in1=rep[:], op=ALU.add)
        f.dve.tensor_tensor(out=imp[:], in0=imp[:], in1=addm[:, qt, :], op=ALU.add)
        f.dve.max(out=m8[:], in_=imp[:])
        f.dve.match_replace(out=rep[:], in_to_replace=m8[:], in_values=imp[:], imm_value=-3e38)
        f.dve.max(out=m8[:], in_=rep[:])
        f.dve.tensor_scalar(out=selbs[g][:, 64:128], in0=imp[:], scalar1=m8[:, 7:8], scalar2=None, op0=ALU.is_ge)

    def topk_apply(qt, g):
        f.pe.transpose(out=selT, in_=selbs[g][:], identity=idb[:])
        f.dve.scalar_tensor_tensor(out=qS[qt % 2][64:128, 4 * g:4 * g + 4, :].kk(("g", g)), in0=selT[64:128, :].re("n (o t) -> n o t", o=1).bc([64, 4, 128]),
                                   scalar=-NEGM, in1=c31f[64:128, 4 * g:4 * g + 4, :], op0=ALU.mult, op1=ALU.add)

    qloads(0)
    for qt in range(NT):
        qt_cur[0] = qt
        if qt + 1 < NT:
            qloads(qt + 1)
        b = qt % 2
        ccs = [0] if qt < 16 else [0, 1]
        for g in range(4):
            qgW = qW[b][:, 4 * g:4 * g + 4, :]
            Ec = [qk_exp(kc[:, g, cc * 128:(cc + 1) * 128], qgW, [(idb[:], cbt[b][:, g, cc, :])]) for cc in ccs]

            def cmp_pv(Ec=Ec, g=g, qt=qt, ccs=ccs):
                po = pso[0]
                nl = len(ccs) - 1
                for r in range(4):
                    for ci, cc in enumerate(ccs):
                        f.pe.matmul(out=po[:, r, 0:65], lhsT=Ec[ci][:, r, :], rhs=vcR[:, g, cc, :], start=(ci == 0), stop=(ci == nl), inc=False)
                for r in range(4):
                    for ci, cc in enumerate(ccs):
                        f.pe.matmul(out=psi[:, r, 0:64], lhsT=Ec[ci][:, r, :], rhs=ovb[:, cc, :], start=(ci == 0), stop=(ci == nl), inc=(ci == nl and r == 3))
                combine(po, 0, g, True)
                if qt >= 8:
                    topk_mask(qt, g)
            push(cmp_pv)
        for g in range(4):
            qgS = qS[b][:, 4 * g:4 * g + 4, :].kk(("g", g))
            qgW = qW[b][:, 4 * g:4 * g + 4, :]
            k0 = max(0, qt - 4)
            for kt in range(k0, qt + 1):
                dl = qt - kt
                bt_ = tb[:, 0, g, :] if dl == 0 else tb[:, 1, g, :] if dl == 1 else w4[:, g, :] if dl == 4 else None
                E_ = qk_exp(kwin[:, g, kt * 128:(kt + 1) * 128], qgW, [(idb[:], bt_)] if bt_ is not None else [])

                def win_pv(E_=E_, kt=kt, k0=k0, qt=qt, g=g):
                    po = pso[2]
                    for r in range(4):
                        f.pe.matmul(out=po[:, r, 0:65], lhsT=E_[:, r, :], rhs=vw1[:, kt, g * 65:(g + 1) * 65], start=(kt == k0), stop=(kt == qt), inc=(kt == qt and r == 3))
                    if kt == qt:
                        combine(po, 2, g, False)
                push(win_pv)
            if qt >= 8:
                topk_apply(qt, g)
            for kt in range(qt + 1):
                dl = qt - kt
                extras = [(idb[:], tb[:, dl, g, :])] if dl < 2 else []
                E_ = qk_exp(kslc[:, g, kt * 128:(kt + 1) * 128], qgS, extras)

                def slc_pv(E_=E_, kt=kt, qt=qt, g=g):
                    po = pso[1]
                    for r in range(4):
                        f.pe.matmul(out=po[:, r, 0:65], lhsT=E_[:, r, :], rhs=vs1[:, kt, g * 65:(g + 1) * 65], start=(kt == 0), stop=(kt == qt), inc=(kt == qt and r == 3))
                    if kt == qt:
                        combine(po, 1, g, False)
                push(slc_pv)
        flush()
        f.act.activation(out=yb[:], in_=acc[:].re("p h d -> p (h d)"), func=AF.Copy)
        for j in range(8):
            f.pe.transpose(out=psT[:, j, :], in_=yb[:, j * 128:(j + 1) * 128], identity=idb[:], inc=(j == 7))
        f.dve.tensor_copy(out=ysb[b][:], in_=psT)
        f.dma(ynT[:, :, qt * 128:(qt + 1) * 128].re("j f t -> f j t"), ysb[b][:])


def mix_phase(f, P, C, I, ysT, ynT, gT, x1):
    wbs = wload(f, P, "wbs", I["w_br_ssd"], 0, 1024, kch=16)
    wbn = wload(f, P, "wbn", I["w_br_nsa"], 0, 1024)
    wo = wload(f, P, "wo", I["w_out"], 0, 1024)
    gpost = bc_load(f, P, "gpost", I["norm_mix_post"][0:1, :], D)
    ys_b = P.sb("ys_b", [128, 16, 512], BF16)
    yn_b = P.sb("yn_b", [128, 8, 512], BF16)
    g_b = P.sb("g_b", [128, 16, 512], BF16)
    mixT = P.sb("mixT", [128, 8, 512], BF16)
    m1 = [P.sb("m1%d" % i, [128, 512]) for i in range(2)]
    m2 = [P.sb("m2%d" % i, [128, 512]) for i in range(2)]
    o_sb = [P.sb("o_sb%d" % i, [128, D]) for i in range(2)]
    xin = [P.sb("xin%d" % i, [128, D]) for i in range(2)]
    res = [P.sb("res%d" % i, [128, D]) for i in range(2)]
    junk = P.sb("junk", [128, D])
    ss = P.sb("ss", [128, 1])
    rs = P.sb("rs", [128, 1])
    ps1 = [P.ps("ps1%d" % i) for i in range(2)]
    ps2 = [P.ps("ps2%d" % i) for i in range(2)]
    ps3 = [P.ps("ps3%d" % i) for i in range(2)]
    n = 0
    def mloads(tb):
        cols = slice(tb * 512, (tb + 1) * 512)
        f.dma(ys_b[:], ysT[:, :, cols].re("j f t -> f j t"))
        f.dma(yn_b[:], ynT[:, :, cols].re("j f t -> f j t"))
        f.dma(g_b[:], gT[:, :, cols].re("j f t -> f j t"))

    mloads(0)
    for tb in range(8):
        for co in range(8):
            a, b_ = ps1[co % 2], ps2[co % 2]
            mm(f, a[:], [(wbs[:, k, co * 128:(co + 1) * 128], ys_b[:, k, :]) for k in range(16)])
            mm(f, b_[:], [(wbn[:, k, co * 128:(co + 1) * 128], yn_b[:, k, :]) for k in range(8)])
            f.dve.tensor_tensor(out=m1[co % 2][:], in0=a[:], in1=g_b[:, co, :], op=ALU.mult)
            f.dve.tensor_tensor(out=m2[co % 2][:], in0=b_[:], in1=g_b[:, 8 + co, :], op=ALU.mult)
            f.pool.tensor_tensor(out=mixT[:, co, :], in0=m1[co % 2][:], in1=m2[co % 2][:], op=ALU.add)
        if tb + 1 < 8:
            mloads(tb + 1)
        for ti in range(4):
            i = tb * 4 + ti
            O = o_sb[n % 2]
            X = xin[n % 2]
            R = res[n % 2]
            n += 1
            f.dma(X[:], I["x"][i * 128:(i + 1) * 128, :])
            for half in range(2):
                ps = ps3[half]
                mm(f, ps[:], [(mixT[:, k, ti * 128:(ti + 1) * 128], wo[:, k, half * 512:(half + 1) * 512]) for k in range(8)])
                f.act.activation(out=O[:, half * 512:(half + 1) * 512], in_=ps[:], func=AF.Copy)
            post_norm_residual(f, P, O, X, gpost, x1[i * 128:(i + 1) * 128, :], (junk, ss, rs, R))


def xattn_phase(f, P, C, I, x1, x2):
    idb = C["idb"]
    wq = wload(f, P, "wq", I["w_xq"], 0, 512)
    wkv = wload(f, P, "wkv", I["w_xkv"], 0, 1024)
    wo = wload(f, P, "wo", I["w_xo"], 0, 1024, kch=4)
    gpre = bc_load(f, P, "gpre", I["norm_x_pre"][0:1, :], D)
    gmem = bc_load(f, P, "gmem", I["norm_mem"][0:1, :], D)
    gpost = bc_load(f, P, "gpost", I["norm_x_post"][0:1, :], D)
    psT = [P.ps("psT%d" % i, [128, 8, 128], BF16) for i in range(2)]
    psA = [P.ps("psA%d" % i) for i in range(2)]
    pso = [P.ps("pso%d" % i, [128, 2, 256]) for i in range(2)]
    ps3 = [P.ps("ps3%d" % i) for i in range(2)]
    rms = Rms(f, P, C, "r", psT)
    mT = P.sb("mT", [128, 8, 256], BF16)
    rms.run(I["mem"], gmem, mT, [0, 1], 0)
    kxT = P.sb("kxT", [128, 4, 256], BF16)
    for h in range(4):
        mm(f, psA[h % 2][:, 0:256], [(wkv[:, k, h * 128:(h + 1) * 128], mT[:, k, :]) for k in range(8)])
        f.dve.tensor_copy(out=kxT[:, h, :], in_=psA[h % 2][:, 0:256])
    vx = P.sb("vx", [128, 2, 4, 129], BF16)
    f.dve.memset(ap=vx[:], constant=1.0)
    for mc in range(2):
        mm(f, psA[mc][:], [(mT[:, k, mc * 128:(mc + 1) * 128], wkv[:, k, 512:1024]) for k in range(8)])
        f.dve.tensor_copy(out=vx[:, mc, :, 0:128], in_=psA[mc][:].re("p (h d) -> p h d", h=4))
    hTb2 = [P.sb("hTb%d" % i, [128, 8, 512], BF16) for i in range(2)]
    qx = P.sb("qx", [128, 4, 512], BF16)
    Ex = P.sb("Ex", [128, 4, 2, 512], BF16)
    den = P.sb("den", [128, 4])
    ob = P.sb("ob", [128, 4, 128], BF16)
    oT = P.sb("oT", [128, 4, 128], BF16)
    o_sb = [P.sb("o_sb%d" % i, [128, D]) for i in range(2)]
    xin = [P.sb("xin%d" % i, [128, D]) for i in range(2)]
    res = [P.sb("res%d" % i, [128, D]) for i in range(2)]
    junk = P.sb("junk", [128, D])
    ss = P.sb("ss", [128, 1])
    rs = P.sb("rs", [128, 1])
    sc = 128 ** -0.5
    n = 0
    rms.run(x1, gpre, hTb2[0], [j for j in range(4)], 0)
    for tb in range(8):
        hTb = hTb2[tb % 2]
        for h in range(4):
            mm(f, psA[h % 2][:], [(wq[:, k, h * 128:(h + 1) * 128], hTb[:, k, :]) for k in range(8)])
            f.dve.tensor_copy(out=qx[:, h, :], in_=psA[h % 2][:])
        for h in range(4):
            for mc in range(2):
                ps = psA[mc]
                mm(f, ps[:], [(kxT[:, h, mc * 128:(mc + 1) * 128], qx[:, h, :])])
                f.act.activation(out=Ex[:, h, mc, :], in_=ps[:], func=AF.Exp, scale=sc)
        if tb + 1 < 8:
            rms.run(x1, gpre, hTb2[(tb + 1) % 2], [(tb + 1) * 4 + j for j in range(4)], 0)
        for ti in range(4):
            i = tb * 4 + ti
            for hp in range(2):
                for hh in range(2):
                    h = 2 * hp + hh
                    mm(f, pso[hp][:, hh, 0:129], [(Ex[:, h, mc, ti * 128:(ti + 1) * 128], vx[:, mc, h, :]) for mc in range(2)])
                f.dve.tensor_scalar(out=den[:, 2 * hp:2 * hp + 2], in0=pso[hp][:, :, 128], scalar1=1e-30, scalar2=None, op0=ALU.max)
                f.dve.reciprocal(out=den[:, 2 * hp:2 * hp + 2], in_=den[:, 2 * hp:2 * hp + 2])
                f.dve.tensor_tensor(out=ob[:, 2 * hp:2 * hp + 2, :], in0=pso[hp][:, :, 0:128],
                                    in1=den[:, 2 * hp:2 * hp + 2].re("p (r o) -> p r o", o=1).bc([128, 2, 128]), op=ALU.mult)
            pt = psT[ti % 2]
            for k in range(4):
                f.pe.transpose(out=pt[:, k, :], in_=ob[:, k, :], identity=idb[:], inc=(k == 3))
            f.dve.tensor_copy(out=oT[:], in_=pt[:, 0:4, :])
            O, X, R = o_sb[n % 2], xin[n % 2], res[n % 2]
            n += 1
            f.dma(X[:], x1[i * 128:(i + 1) * 128, :])
            for half in range(2):
                ps = ps3[half]
                mm(f, ps[:], [(oT[:, k, :], wo[:, k, half * 512:(half + 1) * 512]) for k in range(4)])
                f.act.activation(out=O[:, half * 512:(half + 1) * 512], in_=ps[:], func=AF.Copy)
            post_norm_residual(f, P, O, X, gpost, x2[i * 128:(i + 1) * 128, :], (junk, ss, rs, R))


def ffn_phase(f, P, C, I, x2, out, w2b):
    w1 = P.sb("w1", [128, 8, 4096], BF16)
    for q4 in range(4):
        f.dma(w1[:, :, q4 * 1024:(q4 + 1) * 1024].kk(q4), I["w_ff1"][:, q4 * 1024:(q4 + 1) * 1024].re("(k p) c -> p k c", p=128), q="pool")
    gpre = bc_load(f, P, "gpre", I["norm_ffn_pre"][0:1, :], D)
    gpost = bc_load(f, P, "gpost", I["norm_ffn_post"][0:1, :], D)
    acc = [P.ps("acc%d" % i) for i in range(4)]
    psF = [P.ps("psF%d" % i) for i in range(2)]
    psT = [P.ps("psT%d" % i, [128, 8, 128], BF16) for i in range(2)]
    rms = Rms(f, P, C, "r", psT)
    hTb = [P.sb("hTb%d" % i, [128, 8, 256], BF16) for i in range(2)]
    aT = [P.sb("aT%d" % i, [128, 32, 256], BF16) for i in range(2)]
    rl = [P.sb("rl%d" % i, [128, 256]) for i in range(2)]
    w2c = [P.sb("w2c%d" % i, [128, 4, 1024], BF16) for i in range(4)]
    o_sb = [P.sb("o_sb%d" % i, [128, D]) for i in range(2)]
    xin = [P.sb("xin%d" % i, [128, D]) for i in range(2)]
    res = [P.sb("res%d" % i, [128, D]) for i in range(2)]
    junk = P.sb("junk", [128, D])
    ss = P.sb("ss", [128, 1])
    rs = P.sb("rs", [128, 1])
    for ch in range(8):
        W = w2c[ch % 4]
        f.dma(W[:], I["w_ff2"][ch * 512:(ch + 1) * 512, :].re("(k p) c -> p k c", p=128), q="pool")
        f.dma(w2b[ch * 512:(ch + 1) * 512, :].re("(k p) c -> p k c", p=128), W[:])
    NBK = 16
    cnt = {"n": 0, "w": 0}

    def front(b):
        rms.run(x2, gpre, hTb[b % 2], [b * 2, b * 2 + 1], 0)
        A = aT[b % 2]
        for fc in range(32):
            ps = psF[fc % 2]
            mm(f, ps[:, 0:256], [(w1[:, k, fc * 128:(fc + 1) * 128], hTb[b % 2][:, k, :]) for k in range(8)])
            R_ = rl[fc % 2]
            f.act.activation(out=R_[:], in_=ps[:, 0:256], func=AF.Relu)
            f.dve.tensor_tensor(out=A[:, fc, :], in0=R_[:], in1=R_[:], op=ALU.mult)

    def back(b):
        A = aT[b % 2]
        for ch in range(8):
            W = w2c[cnt["w"] % 4]
            cnt["w"] += 1
            f.dma(W[:], w2b[ch * 512:(ch + 1) * 512, :].re("(k p) c -> p k c", p=128), q="pool")
            for ti in range(2):
                for half in range(2):
                    for k in range(4):
                        f.pe.matmul(out=acc[ti * 2 + half][:], lhsT=A[:, ch * 4 + k, ti * 128:(ti + 1) * 128], rhs=W[:, k, half * 512:(half + 1) * 512],
                                    start=(ch == 0 and k == 0), stop=(ch == 7 and k == 3), inc=(k == 3))
        for ti in range(2):
            i = b * 2 + ti
            n = cnt["n"]
            cnt["n"] += 1
            O, X, R = o_sb[n % 2], xin[n % 2], res[n % 2]
            f.dma(X[:], x2[i * 128:(i + 1) * 128, :])
            for half in range(2):
                f.act.activation(out=O[:, half * 512:(half + 1) * 512], in_=acc[ti * 2 + half][:], func=AF.Copy)
            post_norm_residual(f, P, O, X, gpost, out[i * 128:(i + 1) * 128, :], (junk, ss, rs, R))

    front(0)
    for b in range(NBK):
        if b + 1 < NBK:
            front(b + 1)
        back(b)


_HC = None


def core_inputs(inputs, b):
    global _HC
    if _HC is None:
        _HC = host_consts()
    m = {}
    for n in INPUT_NAMES:
        a = np.asarray(inputs[n])
        if n in ("x", "mem"):
            a = a[b]
        elif a.ndim >= 2 and a.shape[0] == 1:
            a = a[0]
        m[n] = np.ascontiguousarray(a, dtype=np.float32).reshape(SHAPES[n])
    m.update(_HC)
    m.update(host_gather(np.asarray(inputs["rel_bias"], dtype=np.float32)))
    return m


_NC = None


def kernel(**inputs):
    global _NC
    if _NC is None:
        _NC = build("ffn")
    in_maps = [core_inputs(inputs, b) for b in range(8)]
    res = run_bass_kernel_spmd(_NC, in_maps, core_ids=list(range(8)))
    return np.stack([np.asarray(r["out"], dtype=np.float32) for r in res.results], axis=0)
```
